# Optimizing a Trainium2 kernel written in Bass

```python
import jax, jax.numpy as jnp
from jax import lax
import numpy as np

D_MODEL = 2048
BATCH = 1
SEQ = 16384
DEPTH = 4

HEAD_DIM = 128
ROT_DIM = HEAD_DIM // 4
ROPE_THETA = 500000.0
DIL_GROUPS = ((128, 1), (512, 4), (2048, 16))
N_DIL = len(DIL_GROUPS)
A_HEADS = 4
A_WIDTH = A_HEADS * HEAD_DIM
SGU_CHUNK = 128
SGU_GROUPS = 4
SGU_GROUP_DIM = 128
B_WIDTH = SGU_GROUPS * SGU_GROUP_DIM
C_HEADS = 4
C_HEAD_DIM = 256
C_WIDTH = C_HEADS * C_HEAD_DIM
C_CHUNK = 128
C_CONV = 4
N_BRANCH = 3
D_FF = 5632
FFN_CONV = 3
EPS = 1e-6

A_COLS = 3 * N_DIL * A_WIDTH
B_COLS = 2 * B_WIDTH
C_COLS = 4 * C_WIDTH + 2 * C_HEADS
G_COLS = N_BRANCH * D_MODEL
N_IN = A_COLS + B_COLS + C_COLS + G_COLS

kernel_name = 'hybrid_dilated_sgu_mlstm_gated_trunk'


def rms_norm(x, g):
    xf = x.astype(jnp.float32)
    y = xf * lax.rsqrt(jnp.mean(xf * xf, axis=-1, keepdims=True) + EPS)
    return (y * g.astype(jnp.float32)).astype(x.dtype)


def rope_tables(S):
    half = ROT_DIM // 2
    inv = jnp.power(jnp.float32(ROPE_THETA), -jnp.arange(half, dtype=jnp.float32) * (2.0 / ROT_DIM))
    ang = jnp.arange(S, dtype=jnp.float32)[:, None] * inv[None, :]
    return jnp.cos(ang), jnp.sin(ang)


def partial_rope(t, cos, sin):
    half = ROT_DIM // 2
    c = cos[None, :, None, None, :]
    s = sin[None, :, None, None, :]
    tr = t[..., :ROT_DIM].astype(jnp.float32)
    x1, x2 = tr[..., :half], tr[..., half:]
    rot = jnp.concatenate([x1 * c - x2 * s, x2 * c + x1 * s], axis=-1)
    return jnp.concatenate([rot.astype(t.dtype), t[..., ROT_DIM:]], axis=-1)


def causal_dwconv(x, w, b):
    K = w.shape[0]
    y = lax.conv_general_dilated(x, w[:, None, :].astype(x.dtype), window_strides=(1,),
                                 padding=[(K - 1, 0)], dimension_numbers=('NWC', 'WIO', 'NWC'),
                                 feature_group_count=x.shape[-1])
    return y + b.astype(x.dtype)


def dilated_attention(q, k, v, window, dilation):
    B, S, H, dh = q.shape
    blk = window // dilation
    span = dilation * blk
    P = -(-S // span) * span
    L = P // dilation
    nb = L // blk

    def to_sub(t):
        t = jnp.pad(t, ((0, 0), (0, P - S), (0, 0), (0, 0)))
        t = t.reshape(B, L, dilation, H, dh).transpose(0, 2, 1, 3, 4)
        return t.reshape(B, dilation, nb, blk, H, dh)

    def with_prev(t):
        prev = jnp.pad(t, ((0, 0), (0, 0), (1, 0), (0, 0), (0, 0), (0, 0)))[:, :, :-1]
        return jnp.concatenate([prev, t], axis=3)

    qs = to_sub(q)
    kb = with_prev(to_sub(k))
    vb = with_prev(to_sub(v))
    s = jnp.einsum('brnqhd,brnkhd->brnhqk', qs, kb, preferred_element_type=jnp.float32) * (dh ** -0.5)
    i = jnp.arange(blk)[:, None]
    kk = jnp.arange(2 * blk)[None, :]
    band = (kk >= i) & (kk <= i + blk)
    has_prev = (jnp.arange(nb) > 0)[:, None, None] | (kk >= blk)[None]
    mask = band[None] & has_prev
    s = jnp.where(mask[None, None, :, None], s, -jnp.inf)
    lse = jax.nn.logsumexp(s, axis=-1)
    p = jnp.exp(s - lse[..., None])
    o = jnp.einsum('brnhqk,brnkhd->brnqhd', p, vb.astype(jnp.float32))
    o = o.reshape(B, dilation, L, H, dh).transpose(0, 2, 1, 3, 4).reshape(B, P, H, dh)[:, :S]
    lse = lse.transpose(0, 1, 2, 4, 3).reshape(B, dilation, L, H).transpose(0, 2, 1, 3).reshape(B, P, H)[:, :S]
    return o, lse


def spatial_gate(u, v, g_norm, w_s, b_s):
    B, S, _ = u.shape
    v = rms_norm(v, g_norm)
    vc = v.reshape(B, S // SGU_CHUNK, SGU_CHUNK, SGU_GROUPS, SGU_GROUP_DIM)
    w = w_s * jnp.tril(jnp.ones((SGU_CHUNK, SGU_CHUNK), w_s.dtype))[None]
    mixed = jnp.einsum('gts,bnsgc->bntgc', w.astype(vc.dtype), vc) + b_s.T.astype(vc.dtype)[None, None, :, :, None]
    return u * mixed.reshape(B, S, B_WIDTH)


def mlstm(q, k, v, ig, fg):
    B, S, H, d = q.shape
    nc = S // C_CHUNK
    k = k * (d ** -0.5)
    lf = jax.nn.log_sigmoid(fg)

    def chunks(t):
        t = t.reshape((B, nc, C_CHUNK, H) + t.shape[3:])
        return jnp.moveaxis(t, (1, 3), (0, 2))

    causal = jnp.tril(jnp.ones((C_CHUNK, C_CHUNK), bool))

    def step(carry, inp):
        C, n, m = carry
        qc, kc, vc, ic, fc = inp
        b = jnp.cumsum(fc, axis=-1)
        Dm = jnp.where(causal, b[..., :, None] - b[..., None, :] + ic[..., None, :], -jnp.inf)
        m_t = jnp.maximum(b + m[..., None], jnp.max(Dm, axis=-1))
        wmat = jnp.exp(Dm - m_t[..., None])
        inter = jnp.exp(b + m[..., None] - m_t)
        sc = jnp.einsum('bhtd,bhsd->bhts', qc, kc) * wmat
        num = jnp.einsum('bhts,bhsd->bhtd', sc, vc) + inter[..., None] * jnp.einsum('bhtk,bhkv->bhtv', qc, C)
        den = jnp.sum(sc, axis=-1) + inter * jnp.einsum('bhtk,bhk->bht', qc, n)
        h = num / jnp.maximum(jnp.abs(den), jnp.exp(-m_t))[..., None]
        m_new = m_t[..., -1]
        wk = jnp.exp(b[..., -1:] - b + ic - m_new[..., None])
        decay = jnp.exp(b[..., -1] + m - m_new)
        C = decay[..., None, None] * C + jnp.einsum('bhs,bhsk,bhsv->bhkv', wk, kc, vc)
        n = decay[..., None] * n + jnp.einsum('bhs,bhsk->bhk', wk, kc)
        return (C, n, m_new), h

    init = (jnp.zeros((B, H, d, d), jnp.float32), jnp.zeros((B, H, d), jnp.float32), jnp.zeros((B, H), jnp.float32))
    _, hs = lax.scan(step, init, (chunks(q), chunks(k), chunks(v), chunks(ig), chunks(lf)))
    return jnp.moveaxis(hs, (0, 2), (1, 3)).reshape(B, S, H * d)


def setup_inputs(seed: int = 0) -> dict:
    key = jax.random.key(seed)
    ks = jax.random.split(key, 21)

    def nrm(k, shape, s):
        return jax.random.normal(k, shape, jnp.float32) * s

    res_scale = (2 * DEPTH) ** -0.5
    return {
        'x': nrm(ks[0], (BATCH, SEQ, D_MODEL), 1.0),
        'norm1_g': 1.0 + nrm(ks[1], (DEPTH, D_MODEL), 0.05),
        'w_in': nrm(ks[2], (DEPTH, D_MODEL, N_IN), D_MODEL ** -0.5),
        'conv_qk_w': nrm(ks[3], (DEPTH, C_CONV, 2 * C_WIDTH), C_CONV ** -0.5),
        'conv_qk_b': nrm(ks[4], (DEPTH, 2 * C_WIDTH), 0.02),
        'b_igate': nrm(ks[5], (DEPTH, C_HEADS), 0.1),
        'b_fgate': jnp.linspace(3.0, 6.0, C_HEADS, dtype=jnp.float32)[None, :] + nrm(ks[6], (DEPTH, C_HEADS), 0.1),
        'sgu_norm_g': 1.0 + nrm(ks[7], (DEPTH, B_WIDTH), 0.05),
        'sgu_w': nrm(ks[8], (DEPTH, SGU_GROUPS, SGU_CHUNK, SGU_CHUNK), SGU_CHUNK ** -0.5),
        'sgu_b': 1.0 + nrm(ks[9], (DEPTH, SGU_GROUPS, SGU_CHUNK), 0.02),
        'w_branch_a': nrm(ks[10], (DEPTH, A_WIDTH, D_MODEL), A_WIDTH ** -0.5),
        'w_branch_b': nrm(ks[11], (DEPTH, B_WIDTH, D_MODEL), B_WIDTH ** -0.5),
        'w_branch_c': nrm(ks[12], (DEPTH, C_WIDTH, D_MODEL), C_WIDTH ** -0.5),
        'w_out': nrm(ks[13], (DEPTH, D_MODEL, D_MODEL), D_MODEL ** -0.5 * res_scale),
        'norm2_g': 1.0 + nrm(ks[14], (DEPTH, D_MODEL), 0.05),
        'w_up': nrm(ks[15], (DEPTH, D_MODEL, 2 * D_FF), D_MODEL ** -0.5),
        'ffn_conv_w': nrm(ks[16], (DEPTH, FFN_CONV, 2 * D_FF), FFN_CONV ** -0.5),
        'ffn_conv_b': nrm(ks[17], (DEPTH, 2 * D_FF), 0.02),
        'w_down': nrm(ks[18], (DEPTH, D_FF, D_MODEL), D_FF ** -0.5 * res_scale),
        'final_norm_g': 1.0 + nrm(ks[19], (D_MODEL,), 0.05),
    }


def reference(x, norm1_g, w_in, conv_qk_w, conv_qk_b, b_igate, b_fgate, sgu_norm_g, sgu_w, sgu_b,
              w_branch_a, w_branch_b, w_branch_c, w_out, norm2_g, w_up, ffn_conv_w, ffn_conv_b,
              w_down, final_norm_g):
    B, S, D = x.shape
    cos, sin = rope_tables(S)
    splits = [A_COLS, A_COLS + B_COLS, A_COLS + B_COLS + C_COLS]
    for l in range(DEPTH):
        h = rms_norm(x, norm1_g[l])
        z = h @ w_in[l]
        za, zb, zc, zg = jnp.split(z, splits, axis=-1)

        qkv = za.reshape(B, S, 3, N_DIL, A_HEADS, HEAD_DIM)
        qa = partial_rope(qkv[:, :, 0], cos, sin)
        ka = partial_rope(qkv[:, :, 1], cos, sin)
        va = qkv[:, :, 2]
        outs, lses = [], []
        for g, (win, dil) in enumerate(DIL_GROUPS):
            o_g, lse_g = dilated_attention(qa[:, :, g], ka[:, :, g], va[:, :, g], win, dil)
            outs.append(o_g)
            lses.append(lse_g)
        wts = jax.nn.softmax(jnp.stack(lses, axis=0), axis=0)
        y_a = jnp.einsum('gbsh,gbshd->bshd', wts, jnp.stack(outs, axis=0)).reshape(B, S, A_WIDTH).astype(x.dtype)

        u, vb = jnp.split(jax.nn.gelu(zb), 2, axis=-1)
        y_b = spatial_gate(u, vb, sgu_norm_g[l], sgu_w[l], sgu_b[l])

        qk_c, v_c, o_c, gates_c = jnp.split(zc, [2 * C_WIDTH, 3 * C_WIDTH, 4 * C_WIDTH], axis=-1)
        qk_c = jax.nn.silu(causal_dwconv(qk_c, conv_qk_w[l], conv_qk_b[l]))
        q_c, k_c = jnp.split(qk_c, 2, axis=-1)
        gates_c = gates_c.astype(jnp.float32)
        ig = gates_c[..., :C_HEADS] + b_igate[l].astype(jnp.float32)
        fg = gates_c[..., C_HEADS:] + b_fgate[l].astype(jnp.float32)
        hs = (B, S, C_HEADS, C_HEAD_DIM)
        h_c = mlstm(q_c.reshape(hs).astype(jnp.float32), k_c.reshape(hs).astype(jnp.float32),
                    v_c.reshape(hs).astype(jnp.float32), ig, fg)
        y_c = (jax.nn.sigmoid(o_c.astype(jnp.float32)) * h_c).astype(x.dtype)

        gates = jax.nn.sigmoid(zg).reshape(B, S, N_BRANCH, D)
        merged = (gates[:, :, 0] * (y_a @ w_branch_a[l])
                  + gates[:, :, 1] * (y_b @ w_branch_b[l])
                  + gates[:, :, 2] * (y_c @ w_branch_c[l]))
        x = x + merged @ w_out[l]

        h2 = rms_norm(x, norm2_g[l])
        a = causal_dwconv(h2 @ w_up[l], ffn_conv_w[l], ffn_conv_b[l])
        a_gate, a_up = jnp.split(a, 2, axis=-1)
        x = x + (jax.nn.silu(a_gate) * a_up) @ w_down[l]
    return rms_norm(x, final_norm_g)
```

```python
import numpy as np
import ml_dtypes
from contextlib import ExitStack
import concourse.bass as bass
import concourse.mybir as mybir
from concourse.bass_utils import run_bass_kernel_spmd

F32 = mybir.dt.float32
BF16 = mybir.dt.bfloat16
AF = mybir.ActivationFunctionType
ALU = mybir.AluOpType
AX = mybir.AxisListType

NCORES = 8
D = 2048
KD = 16
S = 16384
T = S // NCORES
NCH = T // 128
NTT = T // 512
DEPTH = 4
N_IN = 15880
A_Q0, A_K0, A_V0 = 0, 1536, 3072
B_U0, B_V0 = 4608, 5120
C_Q0, C_K0, C_V0, C_O0, C_G0 = 5632, 6656, 7680, 8704, 9728
G0 = 9736
D_FF = 5632
NFB = D_FF // 128
EPS = 1e-6
DILS = (1, 4, 16)
NEG_INIT = -1.0e4
NEG_SKIP = -3.0e4

SAME_ENGINE_SYNC = True
N_DMA_SEMS = 24


class Key:
    __slots__ = ("name", "lw", "rd")

    def __init__(self, name):
        self.name = name
        self.lw = {}
        self.rd = {}


class Eng:
    def __init__(self, name, obj, sem):
        self.name = name
        self.obj = obj
        self.sem = sem
        self.count = 0
        self.seen = {}


class DSem:
    def __init__(self, name, sem):
        self.name = name
        self.sem = sem
        self.count = 0


class Prog:
    def __init__(self, depth=DEPTH, taps=()):
        self.depth = depth
        self.taps = set(taps)
        self.nc = bass.Bass("TRN2", target_bir_lowering=False)
        self.es = ExitStack()
        nc = self.nc
        self.eng = {}
        self.sems = {}
        for nm, obj in (("pe", nc.tensor), ("act", nc.scalar), ("dve", nc.vector),
                        ("pool", nc.gpsimd), ("sp", nc.sync)):
            sem = self.es.enter_context(nc.semaphore("sem_" + nm))
            self.eng[nm] = Eng(nm, obj, sem)
            self.sems[nm] = sem
        self.dsems = []
        self.qsems = {"sp": [], "pool": [], "act": []}
        self.qi = {"sp": 0, "pool": 0, "act": 0}
        for q, n in (("sp", 20), ("pool", 12), ("act", 2)):
            for i in range(n):
                sem = self.es.enter_context(nc.semaphore("dsem_%s%d" % (q, i)))
                ds = DSem("d%s%d" % (q, i), sem)
                self.dsems.append(ds)
                self.qsems[q].append(ds)
                self.sems[ds.name] = sem
        self.cc_sem = self.es.enter_context(nc.semaphore("ccsem"))
        self.cc = DSem("cc", self.cc_sem)
        self.sems["cc"] = self.cc_sem
        self.keys = {}
        self.uid = 0

    def key(self, name):
        k = self.keys.get(name)
        if k is None:
            k = Key(name)
            self.keys[name] = k
        return k

    def dram_in(self, name, shape, dt=F32):
        return self.nc.dram_tensor(name, list(shape), dt, kind="ExternalInput")

    def dram_out(self, name, shape, dt=F32):
        return self.nc.dram_tensor(name, list(shape), dt, kind="ExternalOutput")

    def dram(self, name, shape, dt=F32):
        if name in self.taps:
            return self.nc.dram_tensor(name, list(shape), dt, kind="ExternalOutput")
        return self.nc.dram_tensor(name, list(shape), dt)

    def sb(self, stack, name, shape, dt=F32):
        self.uid += 1
        return stack.enter_context(self.nc.sbuf_tensor("%s_u%d" % (name, self.uid), list(shape), dt))

    def psum(self, stack, name, shape, dt=F32):
        return stack.enter_context(self.nc.psum_tensor(name, list(shape), dt))

    def _deps(self, E, reads, writes, add=False):
        deps = {}
        for r in reads:
            for n, c in r.lw.items():
                if deps.get(n, 0) < c:
                    deps[n] = c
        for w in writes:
            if not add:
                for n, c in w.lw.items():
                    if deps.get(n, 0) < c:
                        deps[n] = c
            for n, c in w.rd.items():
                if deps.get(n, 0) < c:
                    deps[n] = c
        for n, c in deps.items():
            if n == E.name:
                if E.name == "pe" or not SAME_ENGINE_SYNC:
                    continue
            if E.seen.get(n, 0) >= c:
                continue
            E.obj.wait_ge(self.sems[n], c)
            E.seen[n] = c

    def op(self, eng, reads, writes, fn):
        E = self.eng[eng]
        self._deps(E, reads, writes)
        inst = fn(E.obj)
        E.count += 1
        inst.then_inc(E.sem, 1)
        for r in reads:
            r.rd[E.name] = E.count
        for w in writes:
            w.lw = {E.name: E.count}
            w.rd = {}
        return inst

    def dma(self, q, out, in_, reads, writes, add=False):
        E = self.eng[q]
        s = self.qsems[q][self.qi[q] % len(self.qsems[q])]
        self.qi[q] += 1
        if s.count > E.seen.get(s.name, 0):
            E.obj.wait_ge(s.sem, s.count)
            E.seen[s.name] = s.count
        self._deps(E, reads, writes, add)
        inst = E.obj.dma_start(out=out, in_=in_)
        s.count += 16
        inst.then_inc(s.sem, 16)
        for r in reads:
            r.rd[s.name] = s.count
        for w in writes:
            if add:
                w.lw[s.name] = s.count
            else:
                w.lw = {s.name: s.count}
                w.rd = {}

    def allgather(self, in_t, out_t, reads, writes):
        E = self.eng["pool"]
        self._deps(E, reads, writes)
        inst = E.obj.collective_compute(
            "AllGather", ALU.bypass, replica_groups=[list(range(NCORES))],
            ins=[in_t.ap().opt()], outs=[out_t.ap().opt()])
        self.cc.count += 1
        inst.then_inc(self.cc.sem, 1)
        for r in reads:
            r.rd["cc"] = self.cc.count
        for w in writes:
            w.lw = {"cc": self.cc.count}
            w.rd = {}

    def barrier(self):
        targets = {n: e.count for n, e in self.eng.items()}
        for s in self.dsems:
            targets[s.name] = s.count
        targets["cc"] = self.cc.count
        for E in self.eng.values():
            for n, c in targets.items():
                if c == 0 or n == E.name:
                    continue
                if E.seen.get(n, 0) >= c:
                    continue
                E.obj.wait_ge(self.sems[n], c)
                E.seen[n] = c

    def mm(self, ps_ap, lhsT, rhs, start, stop, reads, writes):
        return self.op("pe", reads, writes,
                       lambda e: e.matmul(ps_ap, lhsT, rhs, start=start, stop=stop))

    def tr(self, ps_ap, in_ap, ident_ap, reads, writes):
        return self.op("pe", reads, writes, lambda e: e.transpose(ps_ap, in_ap, ident_ap))


PV_G1, PV_G2, PV_CQW, PV_CQB, PV_FCW, PV_FCB, PV_GB = 0, 16, 32, 96, 112, 376, 464
PV_L = 472
PV_FIN = DEPTH * PV_L
PV_N = PV_FIN + 16


def _build(self):
    nc = self.nc
    L = self.depth
    es = self.es
    x_in = self.dram_in("x", [T, D])
    wshapes = {"w_in": [D, N_IN], "w_ba": [512, D], "w_bb": [512, D], "w_bc": [1024, D],
               "w_out": [D, D], "w_up": [D, 2 * D_FF], "w_down": [D_FF, D]}
    sshapes = {"sgu_w": [DEPTH, 4, 128, 128], "pvs": [DEPTH, 128, 1024]}
    wdecl = {}
    wfull = {}

    def Wsh(name):
        if name not in wdecl:
            if name in sshapes:
                wdecl[name] = self.dram_in(name, sshapes[name])
            else:
                r, c = wshapes[name]
                wdecl[name] = self.dram_in(name, [DEPTH, r // NCORES, c])
        return wdecl[name]

    def gather_w(name, l):
        r, c = wshapes[name]
        shard = self.nc.dram_tensor("%s_s%d" % (name, l), [r // NCORES, c], F32)
        full = self.nc.dram_tensor("%s_f%d" % (name, l), [r, c], F32)
        ks, kf = self.key("%s_s%d" % (name, l)), self.key("%s_f%d" % (name, l))
        rs_ = r // NCORES
        step = max(1, (1 << 20) // (c * 4))
        for i, r0 in enumerate(range(0, rs_, step)):
            r1 = min(rs_, r0 + step)
            self.dma("sp", shard[r0:r1, :], Wsh(name)[l, r0:r1, :], [], [ks], add=(i > 0))
        self.allgather(shard, full, [ks], [kf])
        wfull[(name, l)] = (full, kf)

    def WF(name, l):
        return wfull[(name, l)]
    self.wdecl = wdecl
    pv_in = self.dram_in("pv", [128, PV_N])
    c_f32 = self.dram_in("c_f32", [128, 4, 128])
    c_rt = self.dram_in("c_rt", [32, 32])
    c_cs = self.dram_in("c_cs", [32, 2, T])
    c_sel = self.dram_in("c_sel", [128, 8])
    c_cm = self.dram_in("c_cm", [128, 1])
    x_halo = self.dram_in("x_halo", [128, KD * 4])
    y_out = self.dram_out("y", [T, D])

    xT = self.dram("xT", [D, T])
    qTa = self.dram("qTa", [12 * 128, T], BF16)
    kTa = self.dram("kTa", [12 * 128, T], BF16)
    Va = self.dram("Va", [12 * 128, T], BF16)
    yT = self.dram("yT", [D, T], BF16)
    qTc = self.dram("qTc", [1024, T])
    kTc = self.dram("kTc", [1024, T])
    vc = self.dram("vc", [T, 1024])
    oTc = self.dram("oTc", [1024, T])
    mT = self.dram("mT", [D, T], BF16)
    xh_in = self.dram("xh_in", [D, 4])
    xh_out = self.dram("xh_out", [NCORES * D, 4])
    kTa_all = self.dram("kTa_all", [NCORES * 12 * 128, T], BF16)
    Va_all = self.dram("Va_all", [NCORES * 12 * 128, T], BF16)
    kTa_prev = self.dram("kTa_prev", [12 * 128, T], BF16)
    Va_prev = self.dram("Va_prev", [12 * 128, T], BF16)
    st_in = self.dram("st_in", [128, 2056])
    st_out = self.dram("st_out", [NCORES * 128, 2056])
    sc_in = self.dram("sc_in", [1, 8])
    sc_out = self.dram("sc_out", [NCORES, 8])

    K = self.key
    sb, op, dma, mm = self.sb, self.op, self.dma, self.mm

    cF = sb(es, "cF", [128, 4, 128])
    ident, Umask, Lmask, ones = cF[:, 0, :], cF[:, 1, :], cF[:, 2, :], cF[:, 3, :]
    cB = sb(es, "cB", [128, 4, 128], BF16)
    identb, Ub, Lb, onesb = cB[:, 0, :], cB[:, 1, :], cB[:, 2, :], cB[:, 3, :]
    Lhb = sb(es, "Lhb", [128, 128], BF16)
    rt = sb(es, "rt", [32, 32])
    sel = sb(es, "sel", [128, 8])
    cm = sb(es, "cm", [128, 1])
    pv = sb(es, "pv_sb", [128, PV_N])
    hT = sb(es, "hT", [128, KD, T], BF16)
    hTh = sb(es, "hTh", [128, KD, 4], BF16)
    gt = sb(es, "gt", [128, NCH, 8])
    epst = sb(es, "epst", [128, 1])
    ps = [self.psum(es, "ps%d" % i, [128, 512]) for i in range(8)]
    kps = [K("ps%d" % i) for i in range(8)]
    kconst = K("const")

    dma("sp", cF[:, :, :], c_f32[:, :, :], [], [kconst])
    dma("sp", rt[:, :], c_rt[:, :], [], [kconst])
    dma("sp", sel[:, :], c_sel[:, :], [], [kconst])
    dma("sp", cm[:, :], c_cm[:, :], [], [kconst])
    dma("sp", pv[:, :], pv_in[:, :], [], [kconst])
    op("dve", [kconst], [K("cB")], lambda e: e.tensor_copy(cB[:, :, :], cF[:, :, :]))
    op("dve", [kconst], [K("Lhb")], lambda e: e.tensor_scalar(
        Lhb[:, :], Lmask, cm[:, 0:1], None, ALU.mult))
    op("dve", [], [K("epst")], lambda e: e.memset(epst[:, :], EPS))
    self.barrier()

    def gather_layer_a(l):
        for nm in ("w_in", "w_ba", "w_bb", "w_bc", "w_out"):
            gather_w(nm, l)

    def gather_layer_b(l):
        for nm in ("w_up", "w_down"):
            gather_w(nm, l)

    gather_layer_a(0)
    gather_layer_b(0)
    pid = nc.gpsimd.partition_id()
    prev = (pid + (NCORES - 1)) % NCORES

    def pcol(l, base, i):
        o = l * PV_L + base + i
        return pv[:, o:o + 1]

    with ExitStack() as st:
        xin = [sb(st, "p0x%d" % i, [128, D]) for i in range(2)]
        xst = [sb(st, "p0s%d" % i, [128, KD, 128]) for i in range(2)]
        for tb in range(NCH):
            b = tb % 2
            kx, ks = K("p0x%d" % b), K("p0s%d" % b)
            dma("sp", xin[b][:, :], x_in[tb * 128:(tb + 1) * 128, :], [], [kx])
            for c4 in range(4):
                pi = (tb * 4 + c4) % 8
                for i in range(4):
                    c = c4 * 4 + i
                    self.tr(ps[pi][:, i * 128:(i + 1) * 128], xin[b][:, c * 128:(c + 1) * 128],
                            ident, [kx, kconst], [kps[pi]])
                engn = "act" if c4 % 2 == 0 else "dve"
                if engn == "act":
                    op("act", [kps[pi]], [ks], lambda e: e.copy(
                        xst[b][:, c4 * 4:(c4 + 1) * 4, :], ps[pi][:, :].rearrange("p (c t) -> p c t", c=4)))
                else:
                    op("dve", [kps[pi]], [ks], lambda e: e.tensor_copy(
                        xst[b][:, c4 * 4:(c4 + 1) * 4, :], ps[pi][:, :].rearrange("p (c t) -> p c t", c=4)))
            dma("sp", xT[:, tb * 128:(tb + 1) * 128].rearrange("(c p) t -> p c t", p=128),
                xst[b][:, :, :], [ks], [K("xT")], add=True)
    self.barrier()

    def norm_phase(l, gbase, first=False):
        with ExitStack() as st:
            xin = [sb(st, "nx%d" % i, [128, KD, 512]) for i in range(2)]
            sq = [sb(st, "nsq%d" % i, [128, 512]) for i in range(3)]
            rs = [sb(st, "nrs%d" % i, [128, 512]) for i in range(2)]
            xhs = sb(st, "nxh", [128, KD, 4])
            sqh = sb(st, "nsqh", [128, KD, 4])
            rsh = sb(st, "nrsh", [128, 4])
            if not first:
                dma("sp", xh_in[:, :], xT[:, T - 4:T], [K("xT")], [K("xh_in")])
                self.allgather(xh_in, xh_out, [K("xh_in")], [K("xh_out")])
            for tt in range(NTT):
                b = tt % 2
                kx = K("nx%d" % b)
                for q4 in range(4):
                    dma("sp", xin[b][:, q4 * 4:(q4 + 1) * 4, :],
                        xT[q4 * 512:(q4 + 1) * 512, tt * 512:(tt + 1) * 512].rearrange(
                            "(c p) t -> p c t", p=128), [K("xT")], [kx], add=(q4 > 0))
                pi = tt % 2
                for c in range(KD):
                    sb_i = c % 3
                    op("act", [kx], [K("nsq%d" % sb_i)], lambda e, c=c, sb_i=sb_i: e.activation(
                        sq[sb_i][:, :], xin[b][:, c, :], AF.Square))
                    mm(ps[pi][:, :], ones, sq[sb_i][:, :], c == 0, c == KD - 1,
                       [K("nsq%d" % sb_i), kconst], [kps[pi]])
                kr = K("nrs%d" % b)
                op("act", [kps[pi], K("epst")], [kr], lambda e: e.activation(
                    rs[b][:, :], ps[pi][:, :], AF.Sqrt, bias=epst[:, 0:1], scale=1.0 / D))
                op("dve", [kr], [kr], lambda e: e.reciprocal(rs[b][:, :], rs[b][:, :]))
                for c in range(KD):
                    op("dve", [kx, kr, kconst], [K("hT%d" % tt)],
                       lambda e, c=c: e.scalar_tensor_tensor(
                           hT[:, c, tt * 512:(tt + 1) * 512], xin[b][:, c, :], pcol(l, gbase, c),
                           rs[b][:, :], ALU.mult, ALU.mult))
            if first:
                dma("sp", xhs[:, :, :].rearrange("p c t -> p (c t)"), x_halo[:, :], [], [K("nxh")])
            else:
                dma("pool", xhs[:, :, :],
                    xh_out[bass.ds(prev * D, D), :].rearrange("(c p) t -> p c t", p=128),
                    [K("xh_out")], [K("nxh")])
            op("dve", [K("nxh"), kconst], [K("nxh")], lambda e: e.tensor_scalar(
                xhs[:, :, :], xhs[:, :, :], cm[:, 0:1], None, ALU.mult))
            op("act", [K("nxh")], [K("nsqh")], lambda e: e.activation(
                sqh[:, :, :], xhs[:, :, :], AF.Square))
            for c in range(KD):
                mm(ps[7][:, 0:4], ones, sqh[:, c, :], c == 0, c == KD - 1,
                   [K("nsqh"), kconst], [kps[7]])
            op("act", [kps[7], K("epst")], [K("nrsh")], lambda e: e.activation(
                rsh[:, :], ps[7][:, 0:4], AF.Sqrt, bias=epst[:, 0:1], scale=1.0 / D))
            op("dve", [K("nrsh")], [K("nrsh")], lambda e: e.reciprocal(rsh[:, :], rsh[:, :]))
            for c in range(KD):
                op("dve", [K("nxh"), K("nrsh"), kconst], [K("hTh")],
                   lambda e, c=c: e.scalar_tensor_tensor(
                       hTh[:, c, :], xhs[:, c, :], pcol(l, gbase, c), rsh[:, :], ALU.mult, ALU.mult))
        self.barrier()


    psrr = [0]

    def next_ps(n=6):
        i = psrr[0] % n
        psrr[0] += 1
        return i

    def proj_phase(l):
        w_in_f, kwin = WF("w_in", l)
        with ExitStack() as st:
            wbuf = [sb(st, "wb%d" % i, [128, KD, 512], BF16) for i in range(3)]
            wrr = [0]
            zs = sb(st, "zs", [128, 512])
            tmp1 = sb(st, "tmp1", [32, 512])
            tmp2 = sb(st, "tmp2", [32, 512])
            stg_b = [sb(st, "stgb%d" % i, [128, 512], BF16) for i in range(2)]
            stg_f = [sb(st, "stgf%d" % i, [128, 512]) for i in range(2)]
            srr = [0, 0]
            zq = sb(st, "zq", [128, 4 + T])
            acc = sb(st, "acc", [128, 512])
            sg = sb(st, "sg", [128, 512])
            uT = sb(st, "uT", [128, 4, 512])
            vg = sb(st, "vg", [128, 512])
            vsq = sb(st, "vsq", [128, 512])
            vn = sb(st, "vn", [128, 512])
            ss = sb(st, "ss", [128, 1])
            ybt = sb(st, "ybt", [128, 4, 128])
            ybs = sb(st, "ybs", [128, 4, 512], BF16)
            wsm = sb(st, "wsm", [128, 4, 128])
            WT = sb(st, "WT", [128, 4, 128])
            pvsb = sb(st, "pvsb", [128, 1024])
            onec = sb(st, "onec", [128, 1])
            ge = sb(st, "ge", [128, NCH, 4])
            op("dve", [], [K("onec")], lambda e: e.memset(onec[:, :], 1.0))
            allh = [K("hT%d" % i) for i in range(NTT)]

            def load_w(col0, ncols):
                b = wrr[0] % 3
                wrr[0] += 1
                kw = K("wb%d" % b)
                for q4 in range(4):
                    dma("pool", wbuf[b][:, q4 * 4:(q4 + 1) * 4, 0:ncols],
                        w_in_f[q4 * 512:(q4 + 1) * 512, col0:col0 + ncols].rearrange(
                            "(c p) n -> p c n", p=128), [kwin], [kw], add=(q4 > 0))
                return wbuf[b], kw

            def fm_tile(wb, kw, m, tt, pi):
                for c in range(KD):
                    mm(ps[pi][:, :], wb[:, c, m * 128:(m + 1) * 128], hT[:, c, tt * 512:(tt + 1) * 512],
                       c == 0, c == KD - 1, [kw, K("hT%d" % tt)], [kps[pi]])

            def tm_tile(wb, kw, ncols, tok_ap_fn, pi):
                for c in range(KD):
                    mm(ps[pi][:, 0:ncols], tok_ap_fn(c), wb[:, c, 0:ncols],
                       c == 0, c == KD - 1, [kw] + allh, [kps[pi]])

            dma("sp", pvsb[:, :], Wsh("pvs")[l, :, :], [], [K("pvsb")])
            dma("sp", wsm[:, :, :], Wsh("sgu_w")[l].rearrange("g t s -> t g s"), [], [K("wsm")])
            for g in range(4):
                op("dve", [K("wsm"), kconst], [K("wsm")], lambda e, g=g: e.tensor_tensor(
                    wsm[:, g, :], wsm[:, g, :], Lmask, ALU.mult))
            for g in range(4):
                self.tr(ps[7][:, g * 128:(g + 1) * 128], wsm[:, g, :], ident, [K("wsm"), kconst], [kps[7]])
            op("act", [kps[7]], [K("WT")], lambda e: e.copy(
                WT[:, :, :], ps[7][:, :].rearrange("p (g t) -> p g t", g=4)))

            blocks = []
            for g in range(3):
                blocks.append(("rope", A_Q0 + g * 512, 512, (qTa, g)))
            for g in range(3):
                blocks.append(("rope", A_K0 + g * 512, 512, (kTa, g)))
            for g in range(3):
                blocks.append(("va", A_V0 + g * 512, 512, g))
            blocks.append(("sgu_u", B_U0, 512, None))
            blocks.append(("sgu_v", B_V0, 512, None))
            for bi in range(2):
                blocks.append(("conv", C_Q0 + bi * 512, 512, (qTc, bi, 0)))
            for bi in range(2):
                blocks.append(("conv", C_K0 + bi * 512, 512, (kTc, bi, 1)))
            for bi in range(2):
                blocks.append(("vc", C_V0 + bi * 512, 512, bi))
            for bi in range(2):
                blocks.append(("osig", C_O0 + bi * 512, 512, bi))
            blocks.append(("gates", C_G0, 8, None))

            loaded = {}

            def ensure(i):
                for j in range(min(i, len(blocks) - 1) + 1):
                    if j not in loaded:
                        loaded[j] = load_w(blocks[j][1], blocks[j][2])

            gx = sb(st, "gx", [128, 512])
            gq = sb(st, "gq", [128, 512])

            def gelu_tanh(dst_ap, pi, dkey):
                op("act", [kps[pi]], [K("gx")], lambda e: e.copy(gx[:, :], ps[pi][:, :]))
                op("act", [K("gx")], [K("gq")], lambda e: e.activation(gq[:, :], gx[:, :], AF.Square))
                op("dve", [K("gq")], [K("gq")], lambda e: e.tensor_scalar(
                    gq[:, :], gq[:, :], 0.044715, 1.0, ALU.mult, ALU.add))
                op("dve", [K("gq"), K("gx")], [K("gq")], lambda e: e.tensor_tensor(
                    gq[:, :], gq[:, :], gx[:, :], ALU.mult))
                op("act", [K("gq")], [K("gq")], lambda e: e.activation(
                    gq[:, :], gq[:, :], AF.Sigmoid, scale=1.5957691216057308))
                op("dve", [K("gq"), K("gx")], [dkey], lambda e: e.tensor_tensor(
                    dst_ap, gq[:, :], gx[:, :], ALU.mult))

            st2 = ExitStack()
            cs = sb(st2, "cs", [32, 2, T])
            dma("sp", cs[:, :, :], c_cs[:, :, :], [], [K("cs")])
            vstage_box = [None]
            ensure(0)
            ensure(1)
            sgu_u = [None]
            for bi_, (kind, col0, ncols, arg) in enumerate(blocks):
                ensure(bi_ + (1 if kind in ("sgu_u", "sgu_v") else 2))
                wb, kw = loaded[bi_]
                if kind == "rope":
                    dst, g = arg
                    for m in range(4):
                        for tt in range(NTT):
                            pi = next_ps()
                            fm_tile(wb, kw, m, tt, pi)
                            op("act", [kps[pi]], [K("zs")], lambda e: e.copy(zs[:, :], ps[pi][:, :]))
                            mm(ps[6][0:32, :], rt[0:32, 0:32], zs[0:32, :], True, True,
                               [K("zs"), kconst], [kps[6]])
                            sbi = srr[0] % 2
                            srr[0] += 1
                            ksb = K("stgb%d" % sbi)
                            op("act", [K("zs")], [ksb], lambda e: e.copy(stg_b[sbi][:, :], zs[:, :]))
                            op("dve", [K("zs"), K("cs")], [K("tmp1")], lambda e: e.tensor_tensor(
                                tmp1[:, :], zs[0:32, :], cs[:, 0, tt * 512:(tt + 1) * 512], ALU.mult))
                            op("dve", [kps[6], K("cs")], [K("tmp2")], lambda e: e.tensor_tensor(
                                tmp2[:, :], ps[6][0:32, :], cs[:, 1, tt * 512:(tt + 1) * 512], ALU.mult))
                            op("dve", [K("tmp1"), K("tmp2")], [ksb], lambda e: e.tensor_tensor(
                                stg_b[sbi][0:32, :], tmp1[:, :], tmp2[:, :], ALU.add))
                            r0 = (g * 4 + m) * 128
                            dma("sp", dst[r0:r0 + 128, tt * 512:(tt + 1) * 512], stg_b[sbi][:, :],
                                [ksb], [K("qkA")], add=True)
                elif kind == "va":
                    g = arg
                    if g == 0:
                        self.barrier()
                        st2.close()
                        vstage_box[0] = sb(st2, "vstage", [128, 4, 16, 128], BF16)
                    vstage = vstage_box[0]
                    d = DILS[g]
                    nb = 16 // d
                    for ti in range(16):
                        r, n = divmod(ti, nb)
                        s0 = r + d * 128 * n
                        pi = next_ps()
                        tm_tile(wb, kw, 512, lambda c: hT[:, c, s0:s0 + d * 127 + 1:d], pi)
                        src_ = ps[pi][:, :].rearrange("p (h c) -> p h c", h=4)
                        if ti % 2 == 0:
                            op("act", [kps[pi]], [K("vstage")], lambda e: e.copy(vstage[:, :, ti, :], src_))
                        else:
                            op("dve", [kps[pi]], [K("vstage")], lambda e: e.tensor_copy(vstage[:, :, ti, :], src_))
                    for h4 in range(4):
                        r0 = (g * 4 + h4) * 128
                        dma("sp", Va[r0:r0 + 128, :], vstage[:, h4, :, :].rearrange("p t c -> p (t c)"),
                            [K("vstage")], [K("Va")], add=True)
                elif kind == "sgu_u":
                    self.barrier()
                    st2.close()
                    sgu_u[0] = (wb, kw)
                elif kind == "sgu_v":
                    wu, kwu = sgu_u[0]
                    for tt in range(NTT):
                        for m in range(4):
                            pi = next_ps()
                            fm_tile(wu, kwu, m, tt, pi)
                            gelu_tanh(uT[:, m, :], pi, K("uT"))
                        for j4 in range(4):
                            j = tt * 4 + j4
                            pi = next_ps()
                            tm_tile(wb, kw, 512, lambda c: hT[:, c, j * 128:(j + 1) * 128], pi)
                            gelu_tanh(vg[:, :], pi, K("vg"))
                            op("act", [K("vg")], [K("vsq")], lambda e: e.activation(
                                vsq[:, :], vg[:, :], AF.Square))
                            op("dve", [K("vsq")], [K("ss")], lambda e: e.reduce_sum(
                                ss[:, :], vsq[:, :], AX.X))
                            op("act", [K("ss"), K("epst")], [K("ss")], lambda e: e.activation(
                                ss[:, :], ss[:, :], AF.Sqrt, bias=epst[:, 0:1], scale=1.0 / 512))
                            op("dve", [K("ss")], [K("ss")], lambda e: e.reciprocal(ss[:, :], ss[:, :]))
                            op("dve", [K("vg"), K("ss"), K("pvsb")], [K("vn")],
                               lambda e: e.scalar_tensor_tensor(
                                   vn[:, :], vg[:, :], ss[:, 0:1], pvsb[:, 512:1024], ALU.mult, ALU.mult))
                            for g in range(4):
                                mm(ps[7][:, g * 128:(g + 1) * 128], vn[:, g * 128:(g + 1) * 128], WT[:, g, :],
                                   True, True, [K("vn"), K("WT")], [kps[7]])
                            op("dve", [kps[7], K("pvsb")], [K("ybt")], lambda e: e.tensor_tensor(
                                ybt[:, :, :], ps[7][:, :].rearrange("p (g t) -> p g t", g=4),
                                pvsb[:, 0:512].rearrange("p (g t) -> p g t", g=4), ALU.add))
                            op("dve", [K("ybt"), K("uT")], [K("ybs")], lambda e: e.tensor_tensor(
                                ybs[:, :, j4 * 128:(j4 + 1) * 128], ybt[:, :, :],
                                uT[:, :, j4 * 128:(j4 + 1) * 128], ALU.mult))
                        dma("sp", yT[512:1024, tt * 512:(tt + 1) * 512].rearrange("(g p) t -> p g t", p=128),
                            ybs[:, :, :], [K("ybs")], [K("yT")], add=True)
                elif kind == "conv":
                    dst, bi, which = arg
                    for m in range(4):
                        cb = bi * 4 + m
                        gcb = which * 8 + cb
                        for c in range(KD):
                            mm(ps[6][:, 0:4], wb[:, c, m * 128:(m + 1) * 128], hTh[:, c, :],
                               c == 0, c == KD - 1, [kw, K("hTh")], [kps[6]])
                        op("act", [kps[6]], [K("zqh")], lambda e: e.copy(zq[:, 0:4], ps[6][:, 0:4]))
                        for tt in range(NTT):
                            pi = next_ps()
                            fm_tile(wb, kw, m, tt, pi)
                            kz = K("zq%d" % tt)
                            kzp = K("zq%d" % (tt - 1)) if tt > 0 else K("zqh")
                            op("act", [kps[pi]], [kz], lambda e: e.copy(
                                zq[:, 4 + tt * 512:4 + (tt + 1) * 512], ps[pi][:, :]))
                            b0 = tt * 512 + 1
                            op("dve", [kz, kzp, kconst], [K("acc")], lambda e: e.tensor_scalar(
                                acc[:, :], zq[:, b0:b0 + 512], pcol(l, PV_CQW, gcb * 4 + 0),
                                pcol(l, PV_CQB, gcb), ALU.mult, ALU.add))
                            for tap in range(1, 4):
                                op("dve", [kz, kzp, kconst, K("acc")], [K("acc")],
                                   lambda e, tap=tap: e.scalar_tensor_tensor(
                                       acc[:, :], zq[:, b0 + tap:b0 + tap + 512],
                                       pcol(l, PV_CQW, gcb * 4 + tap), acc[:, :], ALU.mult, ALU.add))
                            op("act", [K("acc")], [K("sg")], lambda e: e.activation(
                                sg[:, :], acc[:, :], AF.Sigmoid))
                            sbi = srr[1] % 2
                            srr[1] += 1
                            ksf = K("stgf%d" % sbi)
                            op("dve", [K("acc"), K("sg")], [ksf], lambda e: e.scalar_tensor_tensor(
                                stg_f[sbi][:, :], acc[:, :], (0.0625 if which else 1.0), sg[:, :],
                                ALU.mult, ALU.mult))
                            dma("sp", dst[cb * 128:(cb + 1) * 128, tt * 512:(tt + 1) * 512],
                                stg_f[sbi][:, :], [ksf], [K("qkc")], add=True)
                elif kind == "vc":
                    bi = arg
                    for j in range(NCH):
                        pi = next_ps()
                        tm_tile(wb, kw, 512, lambda c: hT[:, c, j * 128:(j + 1) * 128], pi)
                        sbi = srr[1] % 2
                        srr[1] += 1
                        ksf = K("stgf%d" % sbi)
                        if j % 2 == 0:
                            op("act", [kps[pi]], [ksf], lambda e: e.copy(stg_f[sbi][:, :], ps[pi][:, :]))
                        else:
                            op("dve", [kps[pi]], [ksf], lambda e: e.tensor_copy(stg_f[sbi][:, :], ps[pi][:, :]))
                        dma("sp", vc[j * 128:(j + 1) * 128, bi * 512:(bi + 1) * 512], stg_f[sbi][:, :],
                            [ksf], [K("vc")], add=True)
                elif kind == "osig":
                    bi = arg
                    for m in range(4):
                        for tt in range(NTT):
                            pi = next_ps()
                            fm_tile(wb, kw, m, tt, pi)
                            sbi = srr[1] % 2
                            srr[1] += 1
                            ksf = K("stgf%d" % sbi)
                            op("act", [kps[pi]], [ksf], lambda e: e.activation(
                                stg_f[sbi][:, :], ps[pi][:, :], AF.Sigmoid))
                            r0 = (bi * 4 + m) * 128
                            dma("sp", oTc[r0:r0 + 128, tt * 512:(tt + 1) * 512], stg_f[sbi][:, :],
                                [ksf], [K("oTc")], add=True)
                elif kind == "gates":
                    for j in range(NCH):
                        for c in range(KD):
                            mm(ps[7][:, j * 8:(j + 1) * 8], hT[:, c, j * 128:(j + 1) * 128], wb[:, c, 0:8],
                               c == 0, c == KD - 1, [kw] + allh, [kps[7]])
                    for j in range(NCH):
                        op("dve", [kps[7], kconst], [K("gt")], lambda e, j=j: e.tensor_tensor(
                            gt[:, j, :], ps[7][:, j * 8:(j + 1) * 8],
                            pv[:, l * PV_L + PV_GB:l * PV_L + PV_GB + 8], ALU.add))
                    op("act", [K("gt")], [K("ge")], lambda e: e.activation(
                        ge[:, :, :], gt[:, :, 4:8], AF.Exp, scale=-1.0))
                    op("act", [K("ge"), K("onec")], [K("ge")], lambda e: e.activation(
                        ge[:, :, :], ge[:, :, :], AF.Ln, bias=onec[:, 0:1]))
                    op("dve", [K("ge")], [K("gt")], lambda e: e.tensor_scalar(
                        gt[:, :, 4:8], ge[:, :, :], -1.0, None, ALU.mult))
        self.barrier()


    M2 = sb(es, "M2", [128, 256], BF16)
    M2h = sb(es, "M2h", [128, 256], BF16)
    op("dve", [K("cB")], [K("M2")], lambda e: e.tensor_copy(M2[:, 0:128], Ub))
    op("dve", [K("cB")], [K("M2")], lambda e: e.tensor_copy(M2[:, 128:256], Lb))
    op("dve", [K("cB")], [K("M2h")], lambda e: e.tensor_copy(M2h[:, 0:128], Ub))
    op("dve", [K("Lhb")], [K("M2h")], lambda e: e.tensor_copy(M2h[:, 128:256], Lhb[:, :]))

    def attn_phase(l):
        for q3 in range(3):
            dma("pool", kTa_prev[q3 * 512:(q3 + 1) * 512, :],
                kTa_all[bass.ds(prev * 1536, 1536), :][q3 * 512:(q3 + 1) * 512, :],
                [K("kTa_all")], [K("kTa_prev")], add=(q3 > 0))
        for q3 in range(3):
            dma("pool", Va_prev[q3 * 512:(q3 + 1) * 512, :],
                Va_all[bass.ds(prev * 1536, 1536), :][q3 * 512:(q3 + 1) * 512, :],
                [K("Va_all")], [K("Va_prev")], add=(q3 > 0))
        with ExitStack() as st:
            qT = [sb(st, "aq%d" % i, [128, T], BF16) for i in range(2)]
            kT = [sb(st, "ak%d" % i, [128, T], BF16) for i in range(2)]
            kh = [sb(st, "akh%d" % i, [128, T], BF16) for i in range(2)]
            vt = [sb(st, "av%d" % i, [128, 16, 128], BF16) for i in range(2)]
            vh = [sb(st, "avh%d" % i, [128, 16, 128], BF16) for i in range(2)]
            acc = sb(st, "aacc", [128, 2, T])
            ya = sb(st, "aya", [128, T], BF16)
            Et = [sb(st, "aE%d" % i, [128, 256], BF16) for i in range(3)]
            Em = [sb(st, "aEm%d" % i, [128, 256], BF16) for i in range(3)]
            blk = 0
            sc = 1.0 / np.sqrt(128.0)
            pairs = [(h, g) for h in range(4) for g in range(3)]

            def load(i):
                h, g = pairs[i]
                hd = g * 4 + h
                d = DILS[g]
                b = i % 2
                nh = d * 128
                dma("sp", qT[b][:, :], qTa[hd * 128:(hd + 1) * 128, :], [K("qkA")], [K("aq%d" % b)])
                dma("sp", kT[b][:, :], kTa[hd * 128:(hd + 1) * 128, :], [K("qkA")], [K("ak%d" % b)])
                dma("sp", vt[b][:, :, :].rearrange("p t c -> p (t c)"), Va[hd * 128:(hd + 1) * 128, :],
                    [K("Va")], [K("av%d" % b)])
                dma("sp", kh[b][:, 0:nh], kTa_prev[hd * 128:(hd + 1) * 128, T - nh:T],
                    [K("kTa_prev")], [K("akh%d" % b)])
                dma("sp", vh[b][:, :, :].rearrange("p t c -> p (t c)"), Va_prev[hd * 128:(hd + 1) * 128, :],
                    [K("Va_prev")], [K("avh%d" % b)])

            load(0)
            for it, (h, g) in enumerate(pairs):
                if it + 1 < len(pairs):
                    load(it + 1)
                d = DILS[g]
                nb = 16 // d
                b = it % 2
                kq, kk_, kkh, kv, kvh = K("aq%d" % b), K("ak%d" % b), K("akh%d" % b), K("av%d" % b), K("avh%d" % b)
                for r in range(d):
                    for n in range(nb):
                        s0 = r + d * 128 * n
                        sl = slice(s0, s0 + d * 127 + 1, d)
                        pa, pb = blk % 2, 2 + blk % 2
                        eb = blk % 3
                        blk += 1
                        kE, kEm = K("aE%d" % eb), K("aEm%d" % eb)
                        mm(ps[pa][:, 0:128], kT[b][:, sl], qT[b][:, sl], True, True, [kk_, kq], [kps[pa]])
                        if n > 0:
                            sp_ = r + d * 128 * (n - 1)
                            kprev = kT[b][:, sp_:sp_ + d * 127 + 1:d]
                            vprev = vt[b][:, r * nb + n - 1, :]
                            msk, kmsk, kpk, vpk = M2, K("M2"), kk_, kv
                        else:
                            kprev = kh[b][:, r:r + d * 127 + 1:d]
                            vprev = vh[b][:, r * nb + nb - 1, :]
                            msk, kmsk, kpk, vpk = M2h, K("M2h"), kkh, kvh
                        mm(ps[pa][:, 128:256], kprev, qT[b][:, sl], True, True, [kpk, kq], [kps[pa]])
                        op("act", [kps[pa]], [kE], lambda e: e.activation(
                            Et[eb][:, :], ps[pa][:, 0:256], AF.Exp, scale=sc))
                        op("pool", [kE, kmsk], [kEm], lambda e: e.tensor_tensor(
                            Em[eb][:, :], Et[eb][:, :], msk[:, :], ALU.mult))
                        mm(ps[pb][:, 0:128], vt[b][:, r * nb + n, :], Em[eb][:, 0:128], True, False,
                           [kv, kEm], [kps[pb]])
                        mm(ps[pb][:, 0:128], vprev, Em[eb][:, 128:256], False, True, [vpk, kEm], [kps[pb]])
                        mm(ps[pb][:, 128:256], onesb, Em[eb][:, 0:128], True, False, [K("cB"), kEm], [kps[pb]])
                        mm(ps[pb][:, 128:256], onesb, Em[eb][:, 128:256], False, True, [K("cB"), kEm], [kps[pb]])
                        src_ = ps[pb][:, 0:256].rearrange("p (a q) -> p a q", a=2)
                        if g == 0:
                            op("dve", [kps[pb]], [K("aacc")], lambda e: e.tensor_copy(acc[:, :, sl], src_))
                        else:
                            op("dve", [kps[pb], K("aacc")], [K("aacc")], lambda e: e.tensor_tensor(
                                acc[:, :, sl], acc[:, :, sl], src_, ALU.add))
                if g == 2:
                    op("dve", [K("aacc")], [K("aacc")], lambda e: e.reciprocal(acc[:, 1, :], acc[:, 1, :]))
                    op("dve", [K("aacc")], [K("aya")], lambda e: e.tensor_tensor(
                        ya[:, :], acc[:, 0, :], acc[:, 1, :], ALU.mult))
                    dma("sp", yT[h * 128:(h + 1) * 128, :], ya[:, :], [K("aya")], [K("yT")], add=True)
        self.barrier()


    def mlstm_phase(l, which):
        with ExitStack() as st:
            lfc = sb(st, "m_lfc", [128, NCH, 4])
            igc = sb(st, "m_igc", [128, NCH, 4])
            bb = sb(st, "m_b", [128, NCH, 4])
            aa = sb(st, "m_a", [128, NCH, 4])
            Gb = sb(st, "m_Gb", [128, NCH, 4])
            Gtot = sb(st, "m_Gtot", [128, 4])
            mu64 = sb(st, "m_mu64", [64, 1])
            dg = sb(st, "m_dg", [64, 64])
            mub = sb(st, "m_mub", [128, NCH, 4])
            Ms = sb(st, "m_Ms", [128, NCH + 1, 4])
            mup = sb(st, "m_mup", [128, NCH, 4])
            gam = sb(st, "m_gam", [128, NCH, 4])
            wp = sb(st, "m_wp", [128, NCH, 4])
            thr = sb(st, "m_thr", [128, NCH, 4])
            tmpc = sb(st, "m_tmpc", [128, NCH, 4])
            Cst = sb(st, "m_C", [128, 4, 2, 257])
            Cl = sb(st, "m_Cl", [128, 4, 2, 257])
            qch = [sb(st, "m_q%d" % i, [128, 8, 128]) for i in range(2)]
            kch = [sb(st, "m_k%d" % i, [128, 8, 128]) for i in range(2)]
            och = [sb(st, "m_o%d" % i, [128, 8, 128]) for i in range(2)]
            vch = [sb(st, "m_v%d" % i, [128, 1024]) for i in range(2)]
            kk = [sb(st, "m_kk%d" % i, [128, 256]) for i in range(2)]
            Waug = [sb(st, "m_W%d" % i, [128, 257]) for i in range(2)]
            Sm = [sb(st, "m_S%d" % i, [128, 128]) for i in range(2)]
            hc = [sb(st, "m_h%d" % i, [128, 256]) for i in range(2)]
            dd = sb(st, "m_dd", [128, 1])
            ycs = [sb(st, "m_y%d" % i, [128, 8, 128], BF16) for i in range(2)]
            sc1 = sb(st, "m_sc1", [1, 64])
            scb = sb(st, "m_scb", [128, 8, 8])
            Macc = sb(st, "m_Macc", [128, 4])
            t4 = [sb(st, "m_t4%d" % i, [128, 4]) for i in range(5)]
            kS = K("m_small")

            op("dve", [K("gt")], [kS], lambda e: e.tensor_copy(igc[:, :, :], gt[:, :, 0:4]))
            op("dve", [K("gt")], [kS], lambda e: e.tensor_copy(lfc[:, :, :], gt[:, :, 4:8]))
            lf2 = lfc[:, :, :].rearrange("p j h -> p (j h)")
            mm(ps[0][:, 0:64], Umask, lf2, True, True, [kS, kconst], [kps[0]])
            mm(ps[0][:, 64:128], ones, lf2, True, True, [kS, kconst], [kps[0]])
            op("act", [kps[0]], [kS], lambda e: e.copy(bb[:, :, :].rearrange("p j h -> p (j h)"), ps[0][:, 0:64]))
            op("act", [kps[0]], [kS], lambda e: e.copy(Gb[:, :, :].rearrange("p j h -> p (j h)"), ps[0][:, 64:128]))
            op("dve", [kS], [kS], lambda e: e.tensor_tensor(aa[:, :, :], igc[:, :, :], bb[:, :, :], ALU.subtract))
            op("dve", [kS], [kS], lambda e: e.reduce_sum(
                Gtot[:, :], Gb[:, :, :].rearrange("p j h -> p h j"), AX.X))
            self.tr(ps[1][0:64, 0:128], aa[:, :, :].rearrange("p j h -> p (j h)"), ident, [kS, kconst], [kps[1]])
            op("dve", [kps[1]], [kS], lambda e: e.reduce_max(mu64[:, :], ps[1][0:64, 0:128], AX.X))
            op("dve", [kS, kconst], [kS], lambda e: e.tensor_scalar(
                dg[:, :], cF[0:64, 0, 0:64], mu64[:, 0:1], None, ALU.mult))
            mm(ps[1][:, 128:192], cF[0:64, 3, :], dg[:, :], True, True, [kS, kconst], [kps[1]])
            op("act", [kps[1]], [kS], lambda e: e.copy(
                mub[:, :, :].rearrange("p j h -> p (j h)"), ps[1][:, 128:192]))

            def scalar_pass(with_thr):
                for j in range(NCH):
                    op("dve", [kS], [kS], lambda e, j=j: e.tensor_tensor(
                        mup[:, j, :], Ms[:, j, :], mub[:, j, :], ALU.max))
                    op("dve", [kS], [kS], lambda e, j=j: e.tensor_tensor(
                        Ms[:, j + 1, :], Gb[:, j, :], mup[:, j, :], ALU.add))
                op("dve", [kS], [kS], lambda e: e.tensor_tensor(
                    tmpc[:, :, :], Ms[:, 0:NCH, :], mup[:, :, :], ALU.subtract))
                op("act", [kS], [kS], lambda e: e.activation(gam[:, :, :], tmpc[:, :, :], AF.Exp))
                op("dve", [kS], [kS], lambda e: e.tensor_tensor(
                    tmpc[:, :, :], aa[:, :, :], mup[:, :, :], ALU.subtract))
                op("act", [kS], [kS], lambda e: e.activation(wp[:, :, :], tmpc[:, :, :], AF.Exp))
                if with_thr:
                    op("dve", [kS], [kS], lambda e: e.tensor_tensor(
                        tmpc[:, :, :], bb[:, :, :], mup[:, :, :], ALU.add))
                    op("act", [kS], [kS], lambda e: e.activation(thr[:, :, :], tmpc[:, :, :], AF.Exp, scale=-1.0))

            def load_chunk(j, full):
                b = j % 2
                kin = K("m_in%d" % b)
                dma("pool", kch[b][:, :, :], kTc[:, j * 128:(j + 1) * 128].rearrange("(r p) t -> p r t", p=128),
                    [K("qkc")], [kin])
                dma("pool", vch[b][:, :], vc[j * 128:(j + 1) * 128, :], [K("vc")], [kin], add=True)
                if full:
                    dma("pool", qch[b][:, :, :], qTc[:, j * 128:(j + 1) * 128].rearrange("(r p) t -> p r t", p=128),
                        [K("qkc")], [kin], add=True)
                    dma("pool", och[b][:, :, :], oTc[:, j * 128:(j + 1) * 128].rearrange("(r p) t -> p r t", p=128),
                        [K("oTc")], [kin], add=True)
                return kin

            cnt = [0]

            def matrix_pass(full):
                kC = K("m_C")
                kins = {0: load_chunk(0, full)}
                for j in range(NCH):
                    b = j % 2
                    if j + 1 < NCH:
                        kins[j + 1] = load_chunk(j + 1, full)
                    kin = kins[j]
                    for h in range(4):
                        i2 = cnt[0] % 2
                        cnt[0] += 1
                        kkk, kW, kSm, kh_ = K("m_kk%d" % i2), K("m_W%d" % i2), K("m_S%d" % i2), K("m_h%d" % i2)
                        for hf in range(2):
                            self.tr(ps[2][:, hf * 128:(hf + 1) * 128], kch[b][:, h * 2 + hf, :], ident,
                                    [kin, kconst], [kps[2]])
                        op("act", [kps[2]], [kkk], lambda e: e.copy(kk[i2][:, :], ps[2][:, 0:256]))
                        op("dve", [kin, kS], [kW], lambda e: e.tensor_scalar(
                            Waug[i2][:, 0:256], vch[b][:, h * 256:(h + 1) * 256], wp[:, j, h:h + 1], None, ALU.mult))
                        op("dve", [kS], [kW], lambda e: e.tensor_copy(Waug[i2][:, 256:257], wp[:, j, h:h + 1]))
                        op("dve", [kC, kS], [kC], lambda e: e.tensor_scalar(
                            Cst[:, h, :, :], Cst[:, h, :, :], gam[:, j, h:h + 1], None, ALU.mult))
                        if full:
                            for hf in range(2):
                                mm(ps[3][:, 0:128], kch[b][:, h * 2 + hf, :], qch[b][:, h * 2 + hf, :],
                                   hf == 0, hf == 1, [kin], [kps[3]])
                            op("dve", [kps[3], kconst], [kSm], lambda e: e.tensor_tensor(
                                Sm[i2][:, :], ps[3][:, 0:128], Umask, ALU.mult))
                            mm(ps[4][:, 0:257], Sm[i2][:, :], Waug[i2][:, :], True, False, [kSm, kW], [kps[4]])
                            for hf in range(2):
                                mm(ps[4][:, 0:257], qch[b][:, h * 2 + hf, :], Cst[:, h, hf, :], False, hf == 1,
                                   [kin, kC], [kps[4]])
                            op("act", [kps[4]], [K("m_dd")], lambda e: e.activation(
                                dd[:, :], ps[4][:, 256:257], AF.Abs))
                            op("dve", [K("m_dd"), kS], [K("m_dd")], lambda e: e.tensor_scalar(
                                dd[:, :], dd[:, :], thr[:, j, h:h + 1], None, ALU.max))
                            op("dve", [K("m_dd")], [K("m_dd")], lambda e: e.reciprocal(dd[:, :], dd[:, :]))
                            op("act", [kps[4], K("m_dd")], [kh_], lambda e: e.activation(
                                hc[i2][:, :], ps[4][:, 0:256], AF.Identity, scale=dd[:, 0:1]))
                            for hf in range(2):
                                self.tr(ps[5][:, hf * 128:(hf + 1) * 128], hc[i2][:, hf * 128:(hf + 1) * 128], ident,
                                        [kh_, kconst], [kps[5]])
                            ky = K("m_y%d" % b)
                            op("dve", [kps[5], kin], [ky], lambda e: e.tensor_tensor(
                                ycs[b][:, h * 2:h * 2 + 2, :], ps[5][:, 0:256].rearrange("p (a t) -> p a t", a=2),
                                och[b][:, h * 2:h * 2 + 2, :], ALU.mult))
                        for hf in range(2):
                            mm(ps[6 + hf][:, 0:257], kk[i2][:, hf * 128:(hf + 1) * 128], Waug[i2][:, :], True, True,
                               [kkk, kW], [kps[6 + hf]])
                            op("dve", [kps[6 + hf], kC], [kC], lambda e, hf=hf: e.tensor_tensor(
                                Cst[:, h, hf, :], Cst[:, h, hf, :], ps[6 + hf][:, 0:257], ALU.add))
                    if full:
                        dma("sp", yT[1024:2048, j * 128:(j + 1) * 128].rearrange("(r p) t -> p r t", p=128),
                            ycs[b][:, :, :], [K("m_y%d" % b)], [K("yT")], add=True)

            if which == 1:
                op("dve", [], [kS], lambda e: e.memset(Ms[:, 0, :], NEG_INIT))
                op("dve", [], [K("m_C")], lambda e: e.memset(Cst[:, :, :, :], 0.0))
                scalar_pass(False)
                matrix_pass(False)
                dma("sp", st_in[:, :], Cst[:, :, :, :].rearrange("p h a c -> p (h a c)"), [K("m_C")], [K("st_in")])
                dma("sp", sc_in[0:1, 0:4], Ms[0:1, NCH, :], [kS], [K("sc_in")])
                dma("sp", sc_in[0:1, 4:8], Gtot[0:1, :], [kS], [K("sc_in")], add=True)
                self.allgather(st_in, st_out, [K("st_in")], [K("st_out")])
                self.allgather(sc_in, sc_out, [K("sc_in")], [K("sc_out")])
            else:
                dma("sp", sc1[:, :], sc_out[:, :].rearrange("(o r) c -> o (r c)", o=1), [K("sc_out")], [K("m_sc1")])
                mm(ps[0][:, 128:192], cF[0:1, 3, :], sc1[:, :], True, True, [K("m_sc1"), kconst], [kps[0]])
                op("act", [kps[0]], [kS], lambda e: e.copy(
                    scb[:, :, :].rearrange("p r c -> p (r c)"), ps[0][:, 128:192]))
                op("dve", [], [kS], lambda e: e.memset(Macc[:, :], NEG_INIT))
                op("dve", [], [K("m_C")], lambda e: e.memset(Cst[:, :, :, :], 0.0))
                Ge, Ml, tt_, Mn, e12 = t4
                nskip = sb(st, "m_nskip", [128, 8])
                op("dve", [kconst], [kS], lambda e: e.tensor_scalar(
                    nskip[:, :], sel[:, :], -NEG_SKIP, NEG_SKIP, ALU.mult, ALU.add))
                for cp in range(NCORES - 1):
                    dma("sp", Cl[:, :, :, :].rearrange("p h a c -> p (h a c)"), st_out[cp * 128:(cp + 1) * 128, :],
                        [K("st_out")], [K("m_Cl")])
                    selc = sel[:, cp:cp + 1]
                    op("dve", [kS, kconst], [kS], lambda e: e.tensor_scalar(
                        Ge[:, :], scb[:, cp, 4:8], selc, None, ALU.mult))
                    op("dve", [kS, kconst], [kS], lambda e: e.tensor_scalar(
                        Ml[:, :], scb[:, cp, 0:4], selc, nskip[:, cp:cp + 1], ALU.mult, ALU.add))
                    op("dve", [kS], [kS], lambda e: e.tensor_tensor(tt_[:, :], Macc[:, :], Ge[:, :], ALU.add))
                    op("dve", [kS], [kS], lambda e: e.tensor_tensor(Mn[:, :], tt_[:, :], Ml[:, :], ALU.max))
                    op("dve", [kS], [kS], lambda e: e.tensor_tensor(tt_[:, :], tt_[:, :], Mn[:, :], ALU.subtract))
                    op("act", [kS], [kS], lambda e: e.activation(e12[:, :], tt_[:, :], AF.Exp))
                    op("dve", [kS], [kS], lambda e: e.tensor_tensor(Ml[:, :], Ml[:, :], Mn[:, :], ALU.subtract))
                    op("act", [kS], [kS], lambda e: e.activation(Ge[:, :], Ml[:, :], AF.Exp))
                    for h in range(4):
                        op("dve", [K("m_C"), kS], [K("m_C")], lambda e, h=h: e.tensor_scalar(
                            Cst[:, h, :, :], Cst[:, h, :, :], e12[:, h:h + 1], None, ALU.mult))
                        op("dve", [K("m_C"), K("m_Cl"), kS], [K("m_C")], lambda e, h=h: e.scalar_tensor_tensor(
                            Cst[:, h, :, :], Cl[:, h, :, :], Ge[:, h:h + 1], Cst[:, h, :, :], ALU.mult, ALU.add))
                    op("dve", [kS], [kS], lambda e: e.tensor_copy(Macc[:, :], Mn[:, :]))
                op("dve", [kS], [kS], lambda e: e.tensor_copy(Ms[:, 0, :], Macc[:, :]))
                scalar_pass(True)
                matrix_pass(True)
        self.barrier()


    def merge_phase(l):
        w_in_f, kwin = WF("w_in", l)
        wa, kwa = WF("w_ba", l)
        wbb_, kwb = WF("w_bb", l)
        wc, kwc = WF("w_bc", l)
        with ExitStack() as st:
            yTs = sb(st, "g_yT", [128, KD, T], BF16)
            wg = [sb(st, "g_wg%d" % i, [128, KD, 3, 128], BF16) for i in range(2)]
            wp_ = [sb(st, "g_wp%d" % i, [128, KD, 128], BF16) for i in range(2)]
            sgs = [sb(st, "g_sg%d" % i, [128, 512]) for i in range(3)]
            t3 = [sb(st, "g_t%d" % i, [128, 512]) for i in range(3)]
            mst = [sb(st, "g_m%d" % i, [128, 512], BF16) for i in range(2)]
            for q4 in range(4):
                dma("sp", yTs[:, q4 * 4:(q4 + 1) * 4, :],
                    yT[q4 * 512:(q4 + 1) * 512, :].rearrange("(c p) t -> p c t", p=128), [K("yT")], [K("g_yT")],
                    add=(q4 > 0))

            def load(db):
                b = db % 2
                kg, kp = K("g_wg%d" % b), K("g_wp%d" % b)
                for br in range(3):
                    c0 = G0 + br * D + db * 128
                    for q2 in range(2):
                        dma("pool", wg[b][:, q2 * 8:(q2 + 1) * 8, br, :],
                            w_in_f[q2 * 1024:(q2 + 1) * 1024, c0:c0 + 128].rearrange("(c p) n -> p c n", p=128),
                            [kwin], [kg], add=(br + q2 > 0))
                dma("pool", wp_[b][:, 0:4, :], wa[:, db * 128:(db + 1) * 128].rearrange("(c p) n -> p c n", p=128),
                    [kwa], [kp])
                dma("pool", wp_[b][:, 4:8, :], wbb_[:, db * 128:(db + 1) * 128].rearrange("(c p) n -> p c n", p=128),
                    [kwb], [kp], add=True)
                dma("pool", wp_[b][:, 8:16, :], wc[:, db * 128:(db + 1) * 128].rearrange("(c p) n -> p c n", p=128),
                    [kwc], [kp], add=True)
                return kg, kp

            kk_ = {0: load(0)}
            cnt = 0
            for db in range(KD):
                if db + 1 < KD:
                    kk_[db + 1] = load(db + 1)
                kg, kp = kk_[db]
                b = db % 2
                for tt in range(NTT):
                    tsl = slice(tt * 512, (tt + 1) * 512)
                    for br in range(3):
                        pi = br
                        for c in range(KD):
                            mm(ps[pi][:, :], wg[b][:, c, br, :], hT[:, c, tsl], c == 0, c == KD - 1,
                               [kg, K("hT%d" % tt)], [kps[pi]])
                        op("act", [kps[pi]], [K("g_sg%d" % br)], lambda e, br=br, pi=pi: e.activation(
                            sgs[br][:, :], ps[pi][:, :], AF.Sigmoid))
                    rng = ((0, 4), (4, 8), (8, 16))
                    for br in range(3):
                        pi = 3 + br
                        c0, c1 = rng[br]
                        for c in range(c0, c1):
                            mm(ps[pi][:, :], wp_[b][:, c, :], yTs[:, c, tsl], c == c0, c == c1 - 1,
                               [kp, K("g_yT")], [kps[pi]])
                        op("dve", [kps[pi], K("g_sg%d" % br)], [K("g_t%d" % br)], lambda e, br=br, pi=pi: e.tensor_tensor(
                            t3[br][:, :], ps[pi][:, :], sgs[br][:, :], ALU.mult))
                    op("pool", [K("g_t0"), K("g_t1")], [K("g_t0")], lambda e: e.tensor_tensor(
                        t3[0][:, :], t3[0][:, :], t3[1][:, :], ALU.add))
                    mb = cnt % 2
                    cnt += 1
                    op("pool", [K("g_t0"), K("g_t2")], [K("g_m%d" % mb)], lambda e: e.tensor_tensor(
                        mst[mb][:, :], t3[0][:, :], t3[2][:, :], ALU.add))
                    dma("sp", mT[db * 128:(db + 1) * 128, tsl], mst[mb][:, :], [K("g_m%d" % mb)], [K("mT")], add=True)
        self.barrier()

    def down_phase(st, wfull_, kwf, row0, nk, actT, kact, tag, ew=4):
        wd = [sb(st, tag + "_w%d" % i, [128, nk, ew * 128], BF16) for i in range(2)]
        xr = [sb(st, tag + "_x%d" % i, [128, 512]) for i in range(3)]
        ngr = KD // ew

        def load(gi):
            b = gi % 2
            kw = K(tag + "_w%d" % b)
            nq = 4 if nk >= 8 else 2
            bounds = [round(i * nk / nq) for i in range(nq + 1)]
            for q2 in range(nq):
                c0, c1 = bounds[q2], bounds[q2 + 1]
                dma("pool", wd[b][:, c0:c1, :],
                    wfull_[row0 + c0 * 128:row0 + c1 * 128, gi * ew * 128:(gi + 1) * ew * 128].rearrange(
                        "(c p) n -> p c n", p=128), [kwf], [kw], add=(q2 > 0))
            return kw

        kws = {0: load(0)}
        cnt = 0
        for gi in range(ngr):
            if gi + 1 < ngr:
                kws[gi + 1] = load(gi + 1)
            kw = kws[gi]
            b = gi % 2
            for e4 in range(ew):
                eb = gi * ew + e4
                for tt in range(NTT):
                    tsl = slice(tt * 512, (tt + 1) * 512)
                    xb = cnt % 3
                    pi = cnt % 4
                    cnt += 1
                    kx = K(tag + "_x%d" % xb)
                    dma("sp", xr[xb][:, :], xT[eb * 128:(eb + 1) * 128, tsl], [K("xTr")], [kx])
                    for c in range(nk):
                        mm(ps[pi][:, :], wd[b][:, c, e4 * 128:(e4 + 1) * 128], actT[:, c, tsl],
                           c == 0, c == nk - 1, [kw, kact], [kps[pi]])
                    op("dve", [kps[pi], kx], [kx], lambda e: e.tensor_tensor(
                        xr[xb][:, :], xr[xb][:, :], ps[pi][:, :], ALU.add))
                    dma("sp", xT[eb * 128:(eb + 1) * 128, tsl], xr[xb][:, :], [kx], [K("xTw")], add=True)

    def wout_phase(l):
        wo, kwo = WF("w_out", l)
        with ExitStack() as st:
            mTs = sb(st, "o_mT", [128, KD, T], BF16)
            for q4 in range(4):
                dma("sp", mTs[:, q4 * 4:(q4 + 1) * 4, :],
                    mT[q4 * 512:(q4 + 1) * 512, :].rearrange("(c p) t -> p c t", p=128), [K("mT")], [K("o_mT")],
                    add=(q4 > 0))
            down_phase(st, wo, kwo, 0, KD, mTs, K("o_mT"), "o")
        self.barrier()

    NPART = 4
    FBP = NFB // NPART

    def ffn_phase(l):
        wu, kwu = WF("w_up", l)
        wdn, kwdn = WF("w_down", l)
        for part in range(NPART):
            with ExitStack() as st:
                actT = sb(st, "f_act", [128, FBP, T], BF16)
                wub = [sb(st, "f_wu%d" % i, [128, KD, 2, 128], BF16) for i in range(2)]
                zg_ = sb(st, "f_zg", [128, 4 + T])
                zu_ = sb(st, "f_zu", [128, 4 + T])
                ag = sb(st, "f_ag", [128, 512])
                au = sb(st, "f_au", [128, 512])
                sgm = sb(st, "f_sg", [128, 512])

                def load(fi):
                    b = fi % 2
                    fb = part * FBP + fi
                    kw = K("f_wu%d" % b)
                    for gu in range(2):
                        c0 = gu * D_FF + fb * 128
                        for q2 in range(2):
                            dma("pool", wub[b][:, q2 * 8:(q2 + 1) * 8, gu, :],
                                wu[q2 * 1024:(q2 + 1) * 1024, c0:c0 + 128].rearrange("(c p) n -> p c n", p=128),
                                [kwu], [kw], add=(gu + q2 > 0))
                    return kw

                kws = {0: load(0)}
                for fi in range(FBP):
                    if fi + 1 < FBP:
                        kws[fi + 1] = load(fi + 1)
                    kw = kws[fi]
                    b = fi % 2
                    fb = part * FBP + fi
                    for gu, zb_, zk in ((0, zg_, "f_zg"), (1, zu_, "f_zu")):
                        for c in range(KD):
                            mm(ps[6][:, gu * 4:gu * 4 + 4], wub[b][:, c, gu, :], hTh[:, c, :], c == 0, c == KD - 1,
                               [kw, K("hTh")], [kps[6]])
                        op("act", [kps[6]], [K(zk + "h")], lambda e, zb_=zb_, gu=gu: e.copy(
                            zb_[:, 0:4], ps[6][:, gu * 4:gu * 4 + 4]))
                    for tt in range(NTT):
                        for gu, zb_, zk, dst in ((0, zg_, "f_zg", ag), (1, zu_, "f_zu", au)):
                            pi = next_ps()
                            for c in range(KD):
                                mm(ps[pi][:, :], wub[b][:, c, gu, :], hT[:, c, tt * 512:(tt + 1) * 512],
                                   c == 0, c == KD - 1, [kw, K("hT%d" % tt)], [kps[pi]])
                            kz = K("%s%d" % (zk, tt))
                            kzp = K("%s%d" % (zk, tt - 1)) if tt > 0 else K(zk + "h")
                            op("act", [kps[pi]], [kz], lambda e, zb_=zb_, pi=pi: e.copy(
                                zb_[:, 4 + tt * 512:4 + (tt + 1) * 512], ps[pi][:, :]))
                            cbk = gu * NFB + fb
                            b0 = tt * 512 + 2
                            kd = K("f_a%d" % gu)
                            op("dve", [kz, kzp, kconst], [kd], lambda e, zb_=zb_, dst=dst, cbk=cbk: e.tensor_scalar(
                                dst[:, :], zb_[:, b0:b0 + 512], pcol(l, PV_FCW, cbk * 3 + 0),
                                pcol(l, PV_FCB, cbk), ALU.mult, ALU.add))
                            for tap in (1, 2):
                                op("dve", [kz, kzp, kconst, kd], [kd],
                                   lambda e, tap=tap, zb_=zb_, dst=dst, cbk=cbk: e.scalar_tensor_tensor(
                                       dst[:, :], zb_[:, b0 + tap:b0 + tap + 512],
                                       pcol(l, PV_FCW, cbk * 3 + tap), dst[:, :], ALU.mult, ALU.add))
                        op("act", [K("f_a0")], [K("f_sg")], lambda e: e.activation(sgm[:, :], ag[:, :], AF.Sigmoid))
                        op("pool", [K("f_a0"), K("f_sg")], [K("f_a0")], lambda e: e.tensor_tensor(
                            ag[:, :], ag[:, :], sgm[:, :], ALU.mult))
                        op("pool", [K("f_a0"), K("f_a1")], [K("f_act")], lambda e: e.tensor_tensor(
                            actT[:, fi, tt * 512:(tt + 1) * 512], ag[:, :], au[:, :], ALU.mult))
                down_phase(st, wdn, kwdn, part * FBP * 128, FBP, actT, K("f_act"), "fd", ew=2)
            self.barrier()

    def final_phase():
        with ExitStack() as st:
            xin = [sb(st, "fx%d" % i, [128, KD, 512]) for i in range(2)]
            sq = [sb(st, "fsq%d" % i, [128, 512]) for i in range(3)]
            rs = [sb(st, "frs%d" % i, [128, 512]) for i in range(2)]
            ost = [sb(st, "fo%d" % i, [128, D]) for i in range(2)]
            cnt = 0
            for tt in range(NTT):
                b = tt % 2
                kx = K("fx%d" % b)
                for q4 in range(4):
                    dma("sp", xin[b][:, q4 * 4:(q4 + 1) * 4, :],
                        xT[q4 * 512:(q4 + 1) * 512, tt * 512:(tt + 1) * 512].rearrange(
                            "(c p) t -> p c t", p=128), [K("xT")], [kx], add=(q4 > 0))
                pi = 6 + tt % 2
                for c in range(KD):
                    sb_i = c % 3
                    op("act", [kx], [K("fsq%d" % sb_i)], lambda e, c=c, sb_i=sb_i: e.activation(
                        sq[sb_i][:, :], xin[b][:, c, :], AF.Square))
                    mm(ps[pi][:, :], ones, sq[sb_i][:, :], c == 0, c == KD - 1,
                       [K("fsq%d" % sb_i), kconst], [kps[pi]])
                kr = K("frs%d" % b)
                op("act", [kps[pi], K("epst")], [kr], lambda e: e.activation(
                    rs[b][:, :], ps[pi][:, :], AF.Sqrt, bias=epst[:, 0:1], scale=1.0 / D))
                op("dve", [kr], [kr], lambda e: e.reciprocal(rs[b][:, :], rs[b][:, :]))
                for c in range(KD):
                    op("dve", [kx, kr, kconst], [kx], lambda e, c=c: e.scalar_tensor_tensor(
                        xin[b][:, c, :], xin[b][:, c, :], pv[:, PV_FIN + c:PV_FIN + c + 1],
                        rs[b][:, :], ALU.mult, ALU.mult))
                for t4_ in range(4):
                    ob = cnt % 2
                    cnt += 1
                    ko = K("fo%d" % ob)
                    for c4 in range(4):
                        pj = (cnt * 4 + c4) % 6
                        for i in range(4):
                            c = c4 * 4 + i
                            self.tr(ps[pj][:, i * 128:(i + 1) * 128], xin[b][:, c, t4_ * 128:(t4_ + 1) * 128],
                                    ident, [kx, kconst], [kps[pj]])
                        if c4 % 2 == 0:
                            op("act", [kps[pj]], [ko], lambda e: e.copy(ost[ob][:, c4 * 512:(c4 + 1) * 512], ps[pj][:, :]))
                        else:
                            op("dve", [kps[pj]], [ko], lambda e: e.tensor_copy(ost[ob][:, c4 * 512:(c4 + 1) * 512], ps[pj][:, :]))
                    r0 = tt * 512 + t4_ * 128
                    dma("sp", y_out[r0:r0 + 128, :], ost[ob][:, :], [ko], [K("yout")], add=True)
        self.barrier()

    stop = getattr(self, "stop", None)
    for l in range(L):
        norm_phase(l, PV_G1, first=(l == 0))
        if stop == "P1":
            self.barrier()
            return
        proj_phase(l)
        if stop == "P2":
            self.barrier()
            return
        self.allgather(kTa, kTa_all, [K("qkA")], [K("kTa_all")])
        self.allgather(Va, Va_all, [K("Va")], [K("Va_all")])
        mlstm_phase(l, 1)
        if stop == "M1":
            self.barrier()
            return
        if l + 1 < L:
            gather_layer_a(l + 1)
        attn_phase(l)
        if stop == "AT":
            self.barrier()
            return
        mlstm_phase(l, 2)
        if stop == "P3":
            self.barrier()
            return
        merge_phase(l)
        wout_phase(l)
        if stop == "P5":
            self.barrier()
            return
        norm_phase(l, PV_G2)
        if l + 1 < L:
            gather_layer_b(l + 1)
        ffn_phase(l)
    if stop == "L":
        self.barrier()
        return
    final_phase()
    return


Prog.build = _build


def _const_tables():
    ident = np.eye(128, dtype=np.float32)
    U = np.triu(np.ones((128, 128), np.float32))
    Lm = np.tril(np.ones((128, 128), np.float32))
    ones = np.ones((128, 128), np.float32)
    c_f32 = np.ascontiguousarray(np.stack([ident, U, Lm, ones], axis=1))
    R = np.zeros((32, 32), np.float32)
    for j in range(16):
        R[j, j + 16] = -1.0
        R[j + 16, j] = 1.0
    c_rt = np.ascontiguousarray(R.T)
    half = 16
    inv = np.power(np.float32(500000.0), -np.arange(half, dtype=np.float32) * np.float32(2.0 / 32)).astype(np.float32)
    pos = np.arange(S, dtype=np.float32)
    ang = (pos[:, None] * inv[None, :]).astype(np.float32)
    cos = np.cos(ang).astype(np.float32)
    sin = np.sin(ang).astype(np.float32)
    cos32 = np.concatenate([cos, cos], axis=1).T
    sin32 = np.concatenate([sin, sin], axis=1).T
    return c_f32, c_rt, cos32, sin32


def _pack_pv(inp):
    pv = np.zeros((128, PV_N), np.float32)
    pvs = np.zeros((DEPTH, 128, 1024), np.float32)

    def cols(v):
        return np.asarray(v, np.float32).reshape(-1, 128).T

    for l in range(DEPTH):
        o = l * PV_L
        pv[:, o + PV_G1:o + PV_G1 + 16] = cols(inp["norm1_g"][l])
        pv[:, o + PV_G2:o + PV_G2 + 16] = cols(inp["norm2_g"][l])
        cw = np.asarray(inp["conv_qk_w"][l], np.float32)
        for tap in range(4):
            pv[:, o + PV_CQW + tap:o + PV_CQW + 64:4] = cols(cw[tap])
        pv[:, o + PV_CQB:o + PV_CQB + 16] = cols(inp["conv_qk_b"][l])
        fw = np.asarray(inp["ffn_conv_w"][l], np.float32)
        for tap in range(3):
            pv[:, o + PV_FCW + tap:o + PV_FCW + 264:3] = cols(fw[tap])
        pv[:, o + PV_FCB:o + PV_FCB + 88] = cols(inp["ffn_conv_b"][l])
        pv[:, o + PV_GB:o + PV_GB + 4] = np.asarray(inp["b_igate"][l], np.float32)[None, :]
        pv[:, o + PV_GB + 4:o + PV_GB + 8] = np.asarray(inp["b_fgate"][l], np.float32)[None, :]
        pvs[l, :, 0:512] = np.asarray(inp["sgu_b"][l], np.float32).reshape(1, 512)
        pvs[l, :, 512:1024] = np.asarray(inp["sgu_norm_g"][l], np.float32)[None, :]
    pv[:, PV_FIN:PV_FIN + 16] = cols(inp["final_norm_g"])
    return pv, pvs


def make_in_maps(inp):
    c_f32, c_rt, cos32, sin32 = _const_tables()
    pv, pvs = _pack_pv(inp)
    x = np.asarray(inp["x"], np.float32).reshape(S, D)
    names = {"w_in": "w_in", "w_ba": "w_branch_a", "w_bb": "w_branch_b", "w_bc": "w_branch_c",
             "w_out": "w_out", "w_up": "w_up", "w_down": "w_down", "sgu_w": "sgu_w"}
    big = {k: np.asarray(inp[v], np.float32) for k, v in names.items() if v in inp}
    shared = {"pv": pv, "pvs": pvs, "c_f32": c_f32, "c_rt": c_rt}
    if "sgu_w" in big:
        shared["sgu_w"] = big.pop("sgu_w")
    maps = []
    for c in range(NCORES):
        m = dict(shared)
        for k, v in big.items():
            rs = v.shape[1] // NCORES
            m[k] = np.ascontiguousarray(v[:, c * rs:(c + 1) * rs, :])
        m["x"] = np.ascontiguousarray(x[c * T:(c + 1) * T])
        m["c_cs"] = np.ascontiguousarray(np.stack([cos32[:, c * T:(c + 1) * T], sin32[:, c * T:(c + 1) * T]], axis=1))
        selv = (np.arange(NCORES) < c).astype(np.float32)
        m["c_sel"] = np.ascontiguousarray(np.broadcast_to(selv[None, :], (128, NCORES)))
        m["c_cm"] = np.full((128, 1), 0.0 if c == 0 else 1.0, np.float32)
        xh = np.zeros((4, D), np.float32) if c == 0 else x[c * T - 4:c * T]
        m["x_halo"] = np.ascontiguousarray(xh.reshape(4, KD, 128).transpose(2, 1, 0).reshape(128, KD * 4))
        maps.append(m)
    return maps


_PROG_CACHE = {}


def kernel(**inputs):
    if "prog" not in _PROG_CACHE:
        p = Prog()
        p.build()
        _PROG_CACHE["prog"] = p
    p = _PROG_CACHE["prog"]
    maps = make_in_maps(inputs)
    used = set(p.wdecl) | {"x", "pv", "c_f32", "c_rt", "c_cs", "c_sel", "c_cm", "x_halo"}
    maps = [{k: v for k, v in m.items() if k in used} for m in maps]
    res = run_bass_kernel_spmd(p.nc, maps, core_ids=list(range(NCORES)))
    out = np.concatenate([np.asarray(r["y"], np.float32) for r in res.results], axis=0)
    return out.reshape(1, S, D)
```
